# Optimizing a Trainium2 kernel written in Bass

```python
import jax, jax.numpy as jnp
from jax import lax
import numpy as np

D_MODEL = 2048
BATCH = 4
SEQ = 8192
DEPTH = 4

HEAD_DIM = 64
D_MIX = D_MODEL
D_ATTN = D_MIX // 2
D_GMLP = D_MIX - D_ATTN
N_Q_HEADS = D_ATTN // HEAD_DIM
N_KV_HEADS = 4
N_GMLP_HEADS = D_GMLP // HEAD_DIM
WINDOW = 128
CHUNK = 128
RMS_EPS = 1e-6
D_KV = N_KV_HEADS * HEAD_DIM
D_IN = D_ATTN + 2 * D_KV + D_ATTN + 3 * D_GMLP

kernel_name = "hybrid_swa_sink_gmlp_parallel_heads"


def _rmsnorm(x, g):
    xf = x.astype(jnp.float32)
    y = xf * lax.rsqrt(jnp.mean(xf * xf, axis=-1, keepdims=True) + RMS_EPS)
    return (y * g.astype(jnp.float32)).astype(x.dtype)


def _alibi_slopes(n):
    return jnp.asarray(2.0 ** (-8.0 * np.arange(1, n + 1) / n), dtype=jnp.float32)


def _band(t, nb):
    B, S, H, D = t.shape
    tb = t.reshape(B, nb, WINDOW, H, D)
    prev = jnp.pad(tb, ((0, 0), (1, 0), (0, 0), (0, 0), (0, 0)))[:, :-1]
    return jnp.concatenate([prev, tb], axis=2)


def _swa_gqa_sinks(q, k, v, sinks, slopes):
    B, S, Hq, Dh = q.shape
    Hkv = k.shape[2]
    G = Hq // Hkv
    nb = S // WINDOW
    qb = q.reshape(B, nb, WINDOW, Hkv, G, Dh)
    kb = _band(k, nb)
    vb = _band(v, nb)
    scores = jnp.einsum('bnqhgd,bnkhd->bnhgqk', qb, kb).astype(jnp.float32) * (Dh ** -0.5)
    qpos = jnp.arange(WINDOW)[:, None] + WINDOW
    kpos = jnp.arange(2 * WINDOW)[None, :]
    dist = qpos - kpos
    in_window = (dist >= 0) & (dist < WINDOW)
    not_pad = (jnp.arange(nb)[:, None] > 0) | (kpos >= WINDOW)
    mask = in_window[None] & not_pad[:, None, :]
    sl = slopes.reshape(Hkv, G)
    alibi = -sl[:, :, None, None] * dist.astype(jnp.float32)[None, None]
    scores = jnp.where(mask[None, :, None, None], scores + alibi[None, None], -jnp.inf)
    sink = sinks.astype(jnp.float32).reshape(Hkv, G)[None, None, :, :, None, None]
    m = jnp.maximum(jnp.max(scores, axis=-1, keepdims=True), sink)
    p = jnp.exp(scores - m)
    p = p / (jnp.sum(p, axis=-1, keepdims=True) + jnp.exp(sink - m))
    out = jnp.einsum('bnhgqk,bnkhd->bnqhgd', p.astype(v.dtype), vb)
    return out.reshape(B, S, Hq * Dh)


def _chunked_sgu(u, v, w_s, b_s):
    B, S, H, C = v.shape
    nc = S // CHUNK
    vc = v.reshape(B, nc, CHUNK, H, C)
    w = jnp.tril(w_s)
    mixed = jnp.einsum('hts,bnshc->bnthc', w, vc) + jnp.transpose(b_s)[None, None, :, :, None]
    return u * mixed.reshape(B, S, H, C)


def setup_inputs(seed: int = 0) -> dict:
    key = jax.random.key(seed)
    ks = jax.random.split(key, 9)
    f32 = jnp.float32
    x = jax.random.normal(ks[0], (BATCH, SEQ, D_MODEL), f32)
    norm_g = 1.0 + 0.02 * jax.random.normal(ks[1], (DEPTH, D_MODEL), f32)
    w_in = jax.random.normal(ks[2], (DEPTH, D_MODEL, D_IN), f32) * (D_MODEL ** -0.5)
    q_norm = 1.0 + 0.02 * jax.random.normal(ks[3], (DEPTH, HEAD_DIM), f32)
    k_norm = 1.0 + 0.02 * jax.random.normal(ks[4], (DEPTH, HEAD_DIM), f32)
    sinks = 0.5 * jax.random.normal(ks[5], (DEPTH, N_Q_HEADS), f32)
    w_s = jax.random.normal(ks[6], (DEPTH, N_GMLP_HEADS, CHUNK, CHUNK), f32) * (0.5 * CHUNK ** -0.5)
    b_s = 1.0 + 0.02 * jax.random.normal(ks[7], (DEPTH, N_GMLP_HEADS, CHUNK), f32)
    w_out = jax.random.normal(ks[8], (DEPTH, D_MIX, D_MODEL), f32) * (0.5 * D_MIX ** -0.5)
    return {"x": x, "norm_g": norm_g, "w_in": w_in, "q_norm": q_norm, "k_norm": k_norm,
            "sinks": sinks, "w_s": w_s, "b_s": b_s, "w_out": w_out}


def reference(x, norm_g, w_in, q_norm, k_norm, sinks, w_s, b_s, w_out):
    B, S, _ = x.shape
    slopes = _alibi_slopes(N_Q_HEADS)
    sizes = [D_ATTN, D_KV, D_KV, D_ATTN, D_GMLP, D_GMLP, D_GMLP]
    cuts = [int(c) for c in np.cumsum(sizes)[:-1]]
    for l in range(DEPTH):
        h = _rmsnorm(x, norm_g[l])
        proj = jnp.einsum('bsd,de->bse', h, w_in[l])
        q, k, v, g_a, z_u, z_v, g_b = jnp.split(proj, cuts, axis=-1)
        q = _rmsnorm(q.reshape(B, S, N_Q_HEADS, HEAD_DIM), q_norm[l])
        k = _rmsnorm(k.reshape(B, S, N_KV_HEADS, HEAD_DIM), k_norm[l])
        v = v.reshape(B, S, N_KV_HEADS, HEAD_DIM)
        attn = _swa_gqa_sinks(q, k, v, sinks[l], slopes) * jax.nn.silu(g_a)
        z_u = jax.nn.gelu(z_u, approximate=False).reshape(B, S, N_GMLP_HEADS, HEAD_DIM)
        z_v = jax.nn.gelu(z_v, approximate=False).reshape(B, S, N_GMLP_HEADS, HEAD_DIM)
        sgu = _chunked_sgu(z_u, z_v, w_s[l], b_s[l]).reshape(B, S, D_GMLP) * jax.nn.silu(g_b)
        mix = jnp.concatenate([attn, sgu], axis=-1)
        x = x + jnp.einsum('bse,ed->bsd', mix, w_out[l])
    return x
```

```python
import numpy as np
import concourse.bass as bass
import concourse.mybir as mybir
from concourse.bass_utils import run_bass_kernel_spmd

F32 = mybir.dt.float32
BF16 = mybir.dt.bfloat16
AF = mybir.ActivationFunctionType
ALU = mybir.AluOpType


class Buf:
    __slots__ = ("name", "writers", "readers")

    def __init__(self, name):
        self.name = name
        self.writers = {}
        self.readers = {}


class _Op:
    __slots__ = ("eng", "idx", "fn", "deps", "needs_inc", "is_dma", "sem_key", "sem_val", "milestone")


class Sched:
    ENGS = ("pe", "act", "dve", "pool", "sp")

    def __init__(self, nc):
        self.nc = nc
        self.ops = {e: [] for e in self.ENGS}
        self.dma_count = {}
        self.last_dma = {}

    def _record(self, eng, fn, reads, writes, is_dma=False, key=None, waw=True, chain=True):
        op = _Op()
        op.eng = eng
        op.idx = len(self.ops[eng])
        op.fn = fn
        op.is_dma = is_dma
        op.needs_inc = is_dma
        op.sem_key = key
        op.sem_val = 0
        op.milestone = 0
        deps = set()
        wkey = eng if not is_dma else ("dma", key)
        for b in reads:
            deps.update(b.writers.values())
        for b in writes:
            if waw or b.readers:
                deps.update(b.writers.values())
            deps.update(b.readers.values())
        if is_dma:
            self.dma_count[key] = self.dma_count.get(key, 0) + 1
            op.sem_val = 16 * self.dma_count[key]
            if chain and key in self.last_dma:
                deps.add(self.last_dma[key])
            self.last_dma[key] = op
        deps.discard(op)
        if eng == "pe":
            deps = {d for d in deps if d.eng != "pe" or d.is_dma}
        for d in deps:
            d.needs_inc = True
        op.deps = deps
        for b in writes:
            if b.readers:
                b.writers = {}
                b.readers = {}
            b.writers[wkey] = op
        for b in reads:
            b.readers[wkey] = op
        self.ops[eng].append(op)
        return op

    def op(self, eng, fn, reads=(), writes=()):
        return self._record(eng, fn, reads, writes)

    def dma(self, fn, reads=(), writes=(), key=None, eng="sp", waw=True, chain=True):
        return self._record(eng, fn, reads, writes, is_dma=True, key=key, waw=waw, chain=chain)

    def emit(self):
        nc = self.nc
        for e in self.ENGS:
            c = 0
            for op in self.ops[e]:
                if op.is_dma:
                    continue
                if op.needs_inc:
                    c += 1
                op.milestone = c
        from contextlib import ExitStack

        with ExitStack() as st:
            esem = {e: st.enter_context(nc.semaphore("s_" + e)) for e in self.ENGS}
            dsem = {k: st.enter_context(nc.semaphore("d_%s" % str(k))) for k in self.dma_count}
            block = st.enter_context(nc.Block())
            handles = {"pe": block.tensor, "act": block.scalar, "dve": block.vector, "pool": block.gpsimd,
                       "sp": block.sync}

            def gen(ename):
                def body(eng):
                    waited = {}
                    for op in self.ops[ename]:
                        need = {}
                        for d in op.deps:
                            if d.is_dma:
                                s, v = dsem[d.sem_key], d.sem_val
                            else:
                                s, v = esem[d.eng], d.milestone
                            kk = id(s)
                            if waited.get(kk, 0) >= v:
                                continue
                            if kk not in need or need[kk][1] < v:
                                need[kk] = (s, v)
                        for kk, (s, v) in need.items():
                            eng.wait_ge(s, v)
                            waited[kk] = v
                        inst = op.fn(eng)
                        if op.is_dma:
                            inst.then_inc(dsem[op.sem_key], 16)
                        elif op.needs_inc:
                            inst.then_inc(esem[ename], 1)
                    if ename == "sp":
                        for k, cnt in self.dma_count.items():
                            eng.wait_ge(dsem[k], 16 * cnt)
                return body

            for e in self.ENGS:
                if self.ops[e] or e == "sp":
                    handles[e](gen(e))


from contextlib import ExitStack

D = 2048
DIN = 5632
KC = 16
T = 512
NB = 4
EPS = 1e-6
G_GA, G_Q, G_K, G_V, G_ZU, G_ZV, G_GB = 0, 4, 8, 9, 10, 14, 18
NG_IN = 22
NG_OUT = 8


def _host_consts():
    cm = np.zeros((128, 4, 128), np.float32)
    cm[:, 0, :] = np.eye(128, dtype=np.float32)
    cm[:, 1, :] = 1.0
    bd = np.zeros((128, 128), np.float32)
    bd[:64, :64] = 1.0 / 64
    bd[64:, 64:] = 1.0 / 64
    cm[:, 2, :] = bd
    s = np.arange(128)[:, None]
    t = np.arange(128)[None, :]
    cm[:, 3, :] = (t >= s).astype(np.float32)
    slopes = (2.0 ** (-8.0 * np.arange(1, 17) / 16)).astype(np.float32)
    k = np.arange(128)[:, None]
    q = np.arange(128)[None, :]
    wgt = np.zeros((128, 4, 2, 4, 128), np.float32)
    for h in range(4):
        for g in range(4):
            sl = slopes[4 * h + g]
            d_prev = (q - k + 128).astype(np.float32)
            d_own = (q - k).astype(np.float32)
            wgt[:, h, 0, g, :] = np.where(k > q, np.exp(-sl * np.clip(d_prev, 0, 256)), 0.0)
            wgt[:, h, 1, g, :] = np.where(k <= q, np.exp(-sl * np.clip(d_own, 0, 256)), 0.0)
    return cm, wgt.reshape(128, 4096).astype(np.float32)


def _prologue(nc, A, DEPTH):
    S = Sched(nc)
    with ExitStack() as st:
        def sb(name, shape, dt):
            return st.enter_context(nc.sbuf_tensor("p_" + name, shape, dt))

        cm = sb("pcm", [128, 4, 128], F32)
        wsf = sb("wsf", [128, 16, 128], F32)
        wsb = sb("wsb", [128, 16, 128], BF16)
        bfl = sb("bfl", [64, 128], F32)
        bhi = sb("bhi", [64, 128], BF16)
        bdf = sb("bdf", [64, 128], F32)
        blo = sb("blo", [64, 128], BF16)
        wgf = sb("wgf", [128, 4096], F32)
        wgb = sb("wgb", [128, 4096], BF16)
        vst = sb("vst", [128, 128], F32)
        vsb = sb("vsb", [128, 72], F32)
        pst = [st.enter_context(nc.psum_tensor("pps%d" % i, [128, 512], F32)) for i in range(2)]
        Bcm, Bwsf, Bwsb, Bbfl, Bbhi, Bbdf, Bblo, Bwgf, Bwgb, Bvst, Bvsb = [Buf(n) for n in
            "cm wsf wsb bfl bhi bdf blo wgf wgb vst vsb".split()]
        Bps = [Buf("pps%d" % i) for i in range(2)]

        def OP(eng, method, reads, writes, **kw):
            S.op(eng, lambda e: getattr(e, method)(**kw), reads=reads, writes=writes)

        def DMA(key, reads, writes, **kw):
            S.dma(lambda e: e.dma_start(**kw), reads=reads, writes=writes, key=key)

        DMA("cm", [], [Bcm], out=cm[:], in_=A["cmat"][:, :, :])
        S.op("dve", lambda e: e.memset(vst[:], 0.0), reads=[], writes=[Bvst])
        DMA("v0", [], [Bvst], out=vst[0:DEPTH * 16, :], in_=A["norm_g"].rearrange("l (kc p) -> (l kc) p", p=128))
        for hh in range(2):
            DMA("v1%d" % hh, [], [Bvst], out=vst[64:64 + DEPTH, hh * 64:(hh + 1) * 64], in_=A["q_norm"][:, :])
            DMA("v2%d" % hh, [], [Bvst], out=vst[68:68 + DEPTH, hh * 64:(hh + 1) * 64], in_=A["k_norm"][:, :])
        S.op("pe", lambda e: e.transpose(out=pst[0][:, 0:72], in_=vst[0:72, :], identity=cm[0:72, 0, 0:72]),
             reads=[Bvst, Bcm], writes=[Bps[0]])
        OP("dve", "tensor_copy", [Bps[0]], [Bvsb], out=vsb[:], in_=pst[0][:, 0:72])
        DMA("vs", [Bvsb], [], out=A["vec_scr"][:, :], in_=vsb[:])
        DMA("wg", [], [Bwgf], out=wgf[:], in_=A["wgt"][:, :])
        OP("dve", "tensor_copy", [Bwgf], [Bwgb], out=wgb[:], in_=wgf[:])
        DMA("wgs", [Bwgb], [], out=A["wgt_scr"][:, :], in_=wgb[:])
        DMA("bf", [], [Bbfl], out=bfl[0:DEPTH * 16, :], in_=A["b_s"].rearrange("l h t -> (l h) t"))
        nr = DEPTH * 16
        OP("dve", "tensor_copy", [Bbfl], [Bbhi], out=bhi[0:nr, :], in_=bfl[0:nr, :])
        OP("dve", "tensor_tensor", [Bbfl, Bbhi], [Bbdf], out=bdf[0:nr, :], in0=bfl[0:nr, :], in1=bhi[0:nr, :],
           op=ALU.subtract)
        OP("dve", "tensor_copy", [Bbdf], [Bblo], out=blo[0:nr, :], in_=bdf[0:nr, :])
        DMA("bh", [Bbhi], [], out=A["bs_scr"][0].rearrange("l (h t) -> (l h) t", h=16), in_=bhi[0:nr, :])
        DMA("bl", [Bblo], [], out=A["bs_scr"][1].rearrange("l (h t) -> (l h) t", h=16), in_=blo[0:nr, :])
        esc = sb("esc", [64, 1], F32)
        esf = sb("esf", [64, 128], F32)
        esh = sb("esh", [64, 128], BF16)
        esd = sb("esd", [64, 128], F32)
        esl = sb("esl", [64, 128], BF16)
        Besc, Besf, Besh, Besd, Besl = [Buf(n) for n in "esc esf esh esd esl".split()]
        DMA("esc", [], [Besc], out=esc[0:nr, :], in_=A["sinks"].rearrange("l h -> (l h)").unsqueeze(1))
        OP("act", "activation", [Besc], [Besf], out=esf[0:nr, :], in_=esc[0:nr, 0:1].to_broadcast([nr, 128]),
           func=AF.Exp)
        OP("dve", "tensor_copy", [Besf], [Besh], out=esh[0:nr, :], in_=esf[0:nr, :])
        OP("dve", "tensor_tensor", [Besf, Besh], [Besd], out=esd[0:nr, :], in0=esf[0:nr, :], in1=esh[0:nr, :],
           op=ALU.subtract)
        OP("dve", "tensor_copy", [Besd], [Besl], out=esl[0:nr, :], in_=esd[0:nr, :])
        DMA("eh", [Besh], [], out=A["es_scr"][0].rearrange("l (h t) -> (l h) t", h=16), in_=esh[0:nr, :])
        DMA("el", [Besl], [], out=A["es_scr"][1].rearrange("l (h t) -> (l h) t", h=16), in_=esl[0:nr, :])
        for l in range(DEPTH):
            DMA("wsf", [], [Bwsf], out=wsf[:], in_=A["w_s"][l].rearrange("h t s -> t h s"))
            for hq in range(4):
                pb = hq % 2
                for i in range(4):
                    h = hq * 4 + i
                    S.op("pe", lambda e, h=h, i=i, pb=pb: e.transpose(out=pst[pb][:, i * 128:(i + 1) * 128],
                                                                     in_=wsf[:, h, :], identity=cm[:, 0, :]),
                         reads=[Bwsf, Bcm], writes=[Bps[pb]])
                for i in range(4):
                    h = hq * 4 + i
                    OP("dve", "tensor_tensor", [Bps[pb], Bcm], [Bwsb], out=wsb[:, h, :],
                       in0=pst[pb][:, i * 128:(i + 1) * 128], in1=cm[:, 3, :], op=ALU.mult)
            DMA("wss", [Bwsb], [], out=A["ws_scr"][l], in_=wsb[:].rearrange("p h t -> p (h t)"))
        S.emit()


def _main(nc, A, DEPTH, NT):
    S = Sched(nc)
    with ExitStack() as st:
        def sb(name, shape, dt):
            return st.enter_context(nc.sbuf_tensor("m_" + name, shape, dt))

        cm = sb("cm", [128, 4, 128], F32)
        ones_bf = sb("ones_bf", [128, 128], BF16)
        bd_bf = sb("bd_bf", [128, 128], BF16)
        sqh = [sb("sqh%d" % i, [128, T], BF16) for i in range(2)]
        sql = [sb("sql%d" % i, [128, T], BF16) for i in range(2)]
        junk = sb("junk", [128, 1024], BF16)
        ssq = sb("ssq", [128, 8], F32)
        tot = sb("tot", [128, 4], F32)
        dgs = [sb("dgs%d" % i, [128, 128], F32) for i in range(2)]
        wgt = sb("wgt", [128, 4096], BF16)
        vec = sb("vec", [128, 72], F32)
        esr = sb("esr", [2, 2048], BF16)
        hv = sb("hv", [128, 1], F32)
        wsT = sb("wsT", [128, 2048], BF16)
        bs = sb("bs", [2, 2048], BF16)
        kcarry = sb("kcarry", [128, DEPTH, 2, 128], BF16)
        vcarry = sb("vcarry", [128, DEPTH, 256], BF16)
        xT = sb("xT", [128, KC, T], F32)
        hT = sb("hT", [128, KC, T], BF16)
        bigA = sb("bigA", [128, 4096], BF16)
        bigB = sb("bigB", [128, 4096], BF16)
        kT = sb("kT", [128, 2, 640], BF16)
        vv = sb("vv", [128, 5, 256], BF16)
        mixT = sb("mixT", [128, KC, T], BF16)
        wslot = [sb("wslot%d" % i, [128, KC, 256], BF16) for i in range(3)]
        stage = [sb("stage%d" % i, [128, 1024], F32) for i in range(2)]
        wtmp = sb("wtmp", [128, KC, 256], BF16)
        sq = [sb("sq%d" % i, [128, T], F32) for i in range(2)]
        acc = sb("acc", [128, T], F32)
        qln = [sb("qln%d" % i, [128, T], F32) for i in range(2)]
        qr = [sb("qr%d" % i, [128, T], F32) for i in range(2)]
        ex = [sb("ex%d" % i, [128, T], BF16) for i in range(4)]
        pT = [sb("pT%d" % i, [128, T], BF16) for i in range(8)]
        Lc = [sb("Lc%d" % i, [128, T], F32) for i in range(2)]
        tt = [sb("tt%d" % i, [128, T], F32) for i in range(2)]
        gbs = [sb("gbs%d" % i, [128, T], BF16) for i in range(2)]
        t1 = [sb("t1%d" % i, [128, T], F32) for i in range(2)]
        pp = [st.enter_context(nc.psum_tensor("pp%d" % i, [128, 1024], F32)) for i in range(4)]

        ga_s = bigA[:, :].rearrange("p (c t) -> p c t", c=8)
        zv_g = bigA[:, :].rearrange("p (n f) -> p n f", n=4)
        qT = bigB[:, :].rearrange("p (c t) -> p c t", c=8)
        zu_g = qT
        wgt4 = wgt[:, :].rearrange("p (h m n) -> p h m n", h=4, m=2)
        wsT3 = wsT[:, :].rearrange("p (h t) -> p h t", h=16)

        def B(n):
            return Buf(n)

        Bcm, Bones, Bwgt, Bvec, Bes, Bhv, BwsT, Bbs, Bkc, Bvc = [B(n) for n in
            "cm ones wgt vec esr hv wsT bs kcarry vcarry".split()]
        BxT = [B("xT%d" % c) for c in range(KC)]
        BhT = [B("hT%d" % c) for c in range(KC)]
        Bmix = [B("mix%d" % c) for c in range(KC)]
        BbigA, BbigB, BkT, Bvv = B("bigA"), B("bigB"), B("kT"), B("vv")
        Bws = [B("wslot%d" % i) for i in range(3)]
        Bst = [B("stage%d" % i) for i in range(2)]
        Bwtmp = B("wtmp")
        Bscr = [B("scr%d" % i) for i in range(DEPTH)]
        Bsq = [B("sq%d" % i) for i in range(2)]
        Bsqh = [B("sqh%d" % i) for i in range(2)]
        Bsql = [B("sql%d" % i) for i in range(2)]
        Bjunk, Bssq, Btot, Bbd = B("junk"), B("ssq"), B("tot"), B("bd")
        Bdgs = [B("dgs%d" % i) for i in range(2)]
        Bacc = B("acc")
        Bqln = [B("qln%d" % i) for i in range(2)]
        Bqr = [B("qr%d" % i) for i in range(2)]
        Bex = [B("ex%d" % i) for i in range(4)]
        BpT = [B("pT%d" % i) for i in range(8)]
        BLc = [B("Lc%d" % i) for i in range(2)]
        Btt = [B("tt%d" % i) for i in range(2)]
        Bgbs = [B("gbs%d" % i) for i in range(2)]
        Bt1 = [B("t1%d" % i) for i in range(2)]
        PB = [B("bank%d" % i) for i in range(8)]

        def bank(b):
            return pp[b // 2][:, (b % 2) * 512:(b % 2) * 512 + 512]

        hT32 = hT[:].rearrange("p c t -> p (c t)").bitcast(F32)
        NST = 6

        def stg(i):
            return stage[i][:] if i < 2 else hT32[:, (i - 2) * 1024:(i - 1) * 1024]

        def stgB(i):
            return [Bst[i]] if i < 2 else BhT[4 * (i - 2):4 * (i - 1)]

        l0bank = {}
        rot = {"b": 0, "ex": 0, "sq": 0, "q": 0, "L": 0, "g": 0, "st": 0}

        def nextbank():
            b = rot["b"] % 8
            rot["b"] += 1
            return b

        def nextpair():
            if rot["b"] % 2:
                rot["b"] += 1
            b = rot["b"] % 8
            rot["b"] += 2
            return b

        def OP(eng, method, reads, writes, **kw):
            S.op(eng, lambda e: getattr(e, method)(**kw), reads=reads, writes=writes)

        def MM(reads, writes, out, lhsT, rhs, start, stop):
            S.op("pe", lambda e: e.matmul(out, lhsT=lhsT, rhs=rhs, start=start, stop=stop), reads=reads, writes=writes)

        def DMA(key, reads, writes, waw=True, eng="sp", chain=True, **kw):
            S.dma(lambda e: e.dma_start(**kw), reads=reads, writes=writes, key=key, waw=waw, eng=eng, chain=chain)

        DMA("cm", [], [Bcm], out=cm[:], in_=A["cmat"][:, :, :])
        DMA("wgt", [], [Bwgt], out=wgt[:], in_=A["wgt_scr"][:, :])
        DMA("vec", [], [Bvec], out=vec[:], in_=A["vec_scr"][:, :])
        DMA("hv", [], [Bhv], out=hv[:], in_=A["hv"][:, :])
        OP("dve", "tensor_copy", [Bcm], [Bones], out=ones_bf[:], in_=cm[:, 1, :])
        OP("dve", "tensor_copy", [Bcm], [Bbd], out=bd_bf[:], in_=cm[:, 2, :])
        S.op("pool", lambda e: e.memset(kcarry[:].rearrange("p l c k -> p (l c k)"), 0.0), reads=[], writes=[Bkc])
        S.op("pool", lambda e: e.memset(vcarry[:].rearrange("p l f -> p (l f)"), 0.0), reads=[], writes=[Bvc])

        wq = []
        for ti in range(NT + 1):
            for l in range(DEPTH):
                kv_only = (ti == 0 and l == DEPTH - 1)
                for g in range(NG_IN):
                    if kv_only and g not in (G_K, G_V):
                        continue
                    wq.append(("in", l, g))
                if not kv_only:
                    for g in range(NG_OUT):
                        wq.append(("out", l, g))
        wstate = {"issued": 0, "used": 0}

        Bscrg = {}
        conv_list = []
        for (kind, l, g) in wq:
            if (kind, l, g) not in Bscrg:
                Bscrg[(kind, l, g)] = B("scr_%s_%d_%d" % (kind, l, g))
                conv_list.append((kind, l, g))
        cvs = {"i": 0, "n": 0}

        def grp_cols(g):
            if g < G_K:
                o0, gg = (1536, g - G_GA) if g < G_Q else (0, g - G_Q)
                jj, g0 = (2 * gg) // 4, (2 * gg) % 4
                return [(o0 + (8 * jj + 4 * r + g0) * 64, 128) for r in range(2)]
            n0 = g * 256
            return [(n0 - 1024 if n0 < 2560 else n0, 256)]

        def conv_issue(k):
            for _ in range(k):
                if cvs["i"] >= len(conv_list):
                    return
                kind, l, g = conv_list[cvs["i"]]
                cvs["i"] += 1
                runs = grp_cols(g) if kind == "in" else [(g * 256, 256)]
                srcT, dstT = (A["w_in"], A["scr_in"]) if kind == "in" else (A["w_out"], A["scr_out"])
                for (c0, ncol) in runs:
                    key = "cv%d" % (cvs["n"] % 8)
                    cvs["n"] += 1
                    DMA(key, [], [Bscrg[(kind, l, g)]], waw=False, eng="pool", out=dstT[l][:, c0:c0 + ncol],
                        in_=srcT[l][:, c0:c0 + ncol])

        conv_issue(6)

        def issue_weights(upto):
            while wstate["issued"] < min(upto, len(wq)):
                i = wstate["issued"]
                kind, l, g = wq[i]
                s = i % 3
                if kind == "in" and g < G_K:
                    o0, gg = (1536, g - G_GA) if g < G_Q else (0, g - G_Q)
                    jj, g0 = (2 * gg) // 4, (2 * gg) % 4
                    for r in range(2):
                        cs = o0 + (8 * jj + 4 * r + g0) * 64
                        DMA("wt%d" % r, [Bscrg[(kind, l, g)]], [Bwtmp], waw=False, out=wtmp[:, :, r * 128:(r + 1) * 128],
                            in_=A["scr_in"][l][:, cs:cs + 128].rearrange("(kc p) n -> p kc n", p=128))
                    for r in range(2):
                        dstv = wslot[s][:, :, :].rearrange("p kc (c hf d) -> p kc c hf d", c=2, hf=2)[:, :, :, r, :]
                        srcv = wtmp[:, :, r * 128:(r + 1) * 128].rearrange("p kc (c d) -> p kc c d", c=2)
                        OP("dve", "tensor_copy", [Bwtmp], [Bws[s]], out=dstv, in_=srcv)
                elif kind == "in":
                    n0 = g * 256
                    cs = n0 - 1024 if n0 < 2560 else n0
                    DMA("ws%d" % s, [Bscrg[(kind, l, g)]], [Bws[s]], out=wslot[s][:],
                        in_=A["scr_in"][l][:, cs:cs + 256].rearrange("(kc p) n -> p kc n", p=128))
                else:
                    c0 = g * 256
                    DMA("ws%d" % s, [Bscrg[(kind, l, g)]], [Bws[s]], waw=False, out=wslot[s][:, 8:16, :],
                        in_=A["scr_out"][l][1024:2048, c0:c0 + 256].rearrange("(ec p) n -> p ec n", p=128))
                    for half in range(2):
                        for jj in range(2):
                            src = A["scr_out"][l][0:1024, c0:c0 + 256].rearrange(
                                "(jj half g d) n -> jj half d g n", jj=2, half=2, g=4)[jj, half]
                            dst = wslot[s][half * 64:(half + 1) * 64, 4 * jj:4 * jj + 4, :]
                            DMA("ws%d_%d%d" % (s, half, jj), [Bscrg[(kind, l, g)]], [Bws[s]], waw=False, out=dst, in_=src)
                wstate["issued"] += 1

        def next_weights():
            i = wstate["used"]
            wstate["used"] += 1
            issue_weights(i + 3)
            conv_issue(2)
            return i % 3

        issue_weights(2)

        def halo_blocks(ti, l):
            if ti > 0:
                return 0, 0
            return min(l, NB - 1), (l + 1 if l < DEPTH - 1 else NB)

        def tile_layer(ti, l):
            MUL, ADD = ALU.mult, ALU.add
            nk0, nf0 = halo_blocks(ti, l)
            full = nf0 < NB
            ck = slice(nk0 * 128, T)
            cf = slice(nf0 * 128, T)
            if full:
                DMA("wsT", [], [BwsT], out=wsT[:], in_=A["ws_scr"][l])
                DMA("bs", [], [Bbs], out=bs[:], in_=A["bs_scr"][:, l, :])
                DMA("esr", [], [Bes], out=esr[:], in_=A["es_scr"][:, l, :])
            if l == 0:
                b = l0bank["b"]
            else:
                for c in range(KC):
                    if c == 0:
                        OP("act", "activation", [BxT[0]], [Bacc], out=acc[:, ck], in_=xT[:, 0, ck], func=AF.Square)
                    else:
                        s = rot["sq"] % 2
                        rot["sq"] += 1
                        OP("act", "activation", [BxT[c]], [Bsq[s]], out=sq[s][:, ck], in_=xT[:, c, ck], func=AF.Square)
                        OP("pool", "tensor_tensor", [Bacc, Bsq[s]], [Bacc], out=acc[:, ck], in0=acc[:, ck],
                           in1=sq[s][:, ck], op=ADD)
                b = nextbank()
                MM([Bacc, Bcm], [PB[b]], bank(b)[:, ck], cm[:, 1, :], acc[:, ck], True, True)
            OP("act", "activation", [PB[b]], [Bqln[0]], out=qln[0][:, ck], in_=bank(b)[:, ck], func=AF.Ln, scale=1.0 / D,
               bias=EPS)
            OP("act", "activation", [Bqln[0]], [Bqr[0]], out=qr[0][:, ck], in_=qln[0][:, ck], func=AF.Exp, scale=-0.5)
            for c in range(KC):
                OP("dve", "scalar_tensor_tensor", [BxT[c], Bvec, Bqr[0]], [BhT[c]], out=hT[:, c, ck], in0=xT[:, c, ck],
                   scalar=vec[:, l * 16 + c:l * 16 + c + 1], in1=qr[0][:, ck], op0=MUL, op1=MUL)

            def proj_B(s, cc, cols):
                b = nextbank()
                for kc in range(KC):
                    MM([Bws[s], BhT[kc]], [PB[b]], bank(b)[:, cols], wslot[s][:, kc, cc * 128:(cc + 1) * 128],
                       hT[:, kc, cols], kc == 0, kc == KC - 1)
                return b

            def proj_A(s, n0):
                b = nextpair()
                pv = pp[b // 2][:, :].rearrange("p (n f) -> p n f", n=4)
                for n in range(n0, NB):
                    for kc in range(KC):
                        MM([Bws[s], BhT[kc]], [PB[b + n // 2]], pv[:, n, :], hT[:, kc, n * 128:(n + 1) * 128],
                           wslot[s][:, kc, :], kc == 0, kc == KC - 1)
                return b, pv

            if full:
                for g in range(G_GA, G_GA + 4):
                    s = next_weights()
                    for cc in range(2):
                        c = 2 * (g - G_GA) + cc
                        b = proj_B(s, cc, cf)
                        OP("act", "activation", [PB[b]], [BbigA], out=ga_s[:, c, cf], in_=bank(b)[:, cf], func=AF.Silu)
            OP("pool", "tensor_copy", [Bkc], [BkT], out=kT[:, :, 0:128], in_=kcarry[:, l, :, :])
            OP("pool", "tensor_copy", [Bvc], [Bvv], out=vv[:, 0, :], in_=vcarry[:, l, :])
            pend = []

            def qk_finish(item):
                b, s, dst, wb_, gcol, cols = item
                b2 = nextbank()
                MM([Bsqh[s], Bbd], [PB[b2]], bank(b2)[:, cols], bd_bf[:], sqh[s][:, cols], True, False)
                MM([Bsql[s], Bbd], [PB[b2]], bank(b2)[:, cols], bd_bf[:], sql[s][:, cols], False, True)
                u = rot["q"] % 2
                rot["q"] += 1
                OP("act", "activation", [PB[b2]], [Bqln[u]], out=qln[u][:, cols], in_=bank(b2)[:, cols], func=AF.Ln,
                   bias=EPS)
                OP("act", "activation", [Bqln[u]], [Bqr[u]], out=qr[u][:, cols], in_=qln[u][:, cols], func=AF.Exp,
                   scale=-0.5)
                OP("dve", "scalar_tensor_tensor", [PB[b], Bvec, Bqr[u]], [wb_], out=dst, in0=bank(b)[:, cols],
                   scalar=vec[:, gcol:gcol + 1], in1=qr[u][:, cols], op0=MUL, op1=MUL)

            for g in range(G_Q if full else G_K, G_K + 1):
                ws_ = next_weights()
                cols = cf if g < G_K else ck
                for cc in range(2):
                    b = proj_B(ws_, cc, cols)
                    s = rot["sq"] % 2
                    rot["sq"] += 1
                    OP("act", "activation", [PB[b]], [Bsq[s]], out=sq[s][:, cols], in_=bank(b)[:, cols], func=AF.Square)
                    OP("dve", "tensor_copy", [Bsq[s]], [Bsqh[s]], out=sqh[s][:, cols], in_=sq[s][:, cols])
                    OP("dve", "tensor_tensor", [Bsq[s], Bsqh[s]], [Bsql[s]], out=sql[s][:, cols], in0=sq[s][:, cols],
                       in1=sqh[s][:, cols], op=ALU.subtract)
                    if g < G_K:
                        c = 2 * (g - G_Q) + cc
                        item = (b, s, qT[:, c, cols], BbigB, 64 + l, cols)
                    else:
                        item = (b, s, kT[:, cc, 128 + nk0 * 128:640], BkT, 68 + l, cols)
                    if pend:
                        qk_finish(pend.pop())
                    pend.append(item)
            ws_ = next_weights()
            b, pv = proj_A(ws_, nk0)
            qk_finish(pend.pop())
            OP("dve", "tensor_copy", [PB[b], PB[b + 1]], [Bvv], out=vv[:, 1 + nk0:5, :], in_=pv[:, nk0:4, :])
            OP("pool", "tensor_copy", [BkT], [Bkc], out=kcarry[:, l, :, :], in_=kT[:, :, 512:640])
            OP("pool", "tensor_copy", [Bvv], [Bvc], out=vcarry[:, l, :], in_=vv[:, 4, :])
            if not full:
                return

            def att_S(n, jj):
                banks = [None] * 4
                for m in range(2):
                    for hf in range(2):
                        rows = slice(64 * hf, 64 * hf + 64)
                        b = nextbank()
                        MM([BkT, BbigB], [PB[b]], bank(b), kT[rows, jj, (n + m) * 128:(n + m + 1) * 128],
                           qT[rows, 4 * jj:4 * jj + 4, n * 128:(n + 1) * 128], True, True)
                        banks[2 * hf + m] = b
                return banks

            def att_E(n, jj, banks):
                idx = [None] * 4
                for m in range(2):
                    for hf in range(2):
                        h = 2 * jj + hf
                        b = banks[2 * hf + m]
                        u = rot["ex"] % 4
                        w = rot["ex"] % 8
                        rot["ex"] += 1
                        OP("act", "activation", [PB[b]], [Bex[u]], out=ex[u][:], in_=bank(b), func=AF.Exp, scale=0.125)
                        if ti == 1 and n == 0 and m == 0:
                            OP("dve", "scalar_tensor_tensor", [Bex[u], Bhv, Bwgt], [BpT[w]], out=pT[w][:], in0=ex[u][:],
                               scalar=hv[:, 0:1], in1=wgt4[:, h, m, :], op0=MUL, op1=MUL)
                        else:
                            OP("dve", "tensor_tensor", [Bex[u], Bwgt], [BpT[w]],
                               out=pT[w][:], in0=ex[u][:], in1=wgt4[:, h, m, :], op=MUL)
                        idx[2 * hf + m] = w
                return idx

            def att_OLmm(n, jj, idx):
                bO = nextbank()
                bL = nextbank()
                for m in range(2):
                    for hf in range(2):
                        rows = slice(64 * hf, 64 * hf + 64)
                        w = idx[2 * hf + m]
                        MM([Bvv, BpT[w]], [PB[bO]], bank(bO)[rows, :],
                           vv[:, n + m, jj * 128 + 64 * hf:jj * 128 + 64 * hf + 64], pT[w][:], m == 0, m == 1)
                for m in range(2):
                    for hf in range(2):
                        rows = slice(64 * hf, 64 * hf + 64)
                        w = idx[2 * hf + m]
                        MM([Bones, BpT[w]], [PB[bL]], bank(bL)[rows, :], ones_bf[:, 0:64], pT[w][:], m == 0, False)
                for hf in range(2):
                    rows = slice(64 * hf, 64 * hf + 64)
                    h = 2 * jj + hf
                    MM([Bones, Bes], [PB[bL]], bank(bL)[rows, :], ones_bf[0:2, 0:64], esr[0:2, h * 512:(h + 1) * 512],
                       False, True)
                return bO, bL

            def att_post(n, jj, mm):
                bO, bL = mm
                u = rot["L"] % 2
                rot["L"] += 1
                OP("act", "activation", [PB[bL]], [BLc[u]], out=Lc[u][:], in_=bank(bL), func=AF.Ln)
                OP("act", "activation", [BLc[u]], [BLc[u]], out=Lc[u][:], in_=Lc[u][:], func=AF.Exp, scale=-1.0)
                OP("dve", "tensor_tensor", [PB[bO], BLc[u]], [Btt[u]], out=tt[u][:], in0=bank(bO), in1=Lc[u][:], op=MUL)
                OP("pool", "tensor_tensor", [Btt[u], BbigA], Bmix[4 * jj:4 * jj + 4],
                   out=mixT[:, 4 * jj:4 * jj + 4, n * 128:(n + 1) * 128],
                   in0=tt[u][:].rearrange("p (g q) -> p g q", g=4),
                   in1=ga_s[:, 4 * jj:4 * jj + 4, n * 128:(n + 1) * 128], op=MUL)

            its = [(n, jj) for n in range(nf0, NB) for jj in range(2)]
            Sb = {0: att_S(*its[0])}
            if len(its) > 1:
                Sb[1] = att_S(*its[1])
            Eb = {0: att_E(*its[0], Sb[0])}
            for i, (n, jj) in enumerate(its):
                mm = att_OLmm(n, jj, Eb[i])
                if i + 1 < len(its):
                    Eb[i + 1] = att_E(*its[i + 1], Sb[i + 1])
                if i + 2 < len(its):
                    Sb[i + 2] = att_S(*its[i + 2])
                att_post(n, jj, mm)

            for g in range(G_ZU, G_ZU + 4):
                ws_ = next_weights()
                for cc in range(2):
                    c = 2 * (g - G_ZU) + cc
                    b = proj_B(ws_, cc, cf)
                    OP("act", "activation", [PB[b]], [BbigB], out=zu_g[:, c, cf], in_=bank(b)[:, cf], func=AF.Gelu)
            for g in range(G_ZV, G_ZV + 4):
                ws_ = next_weights()
                b, pv = proj_A(ws_, nf0)
                f0 = (g - G_ZV) * 256
                OP("act", "activation", [PB[b], PB[b + 1]], [BbigA], out=zv_g[:, nf0:4, f0:f0 + 256], in_=pv[:, nf0:4, :],
                   func=AF.Gelu)
            nbf = NB - nf0
            for g in range(G_GB, G_GB + 4):
                ws_ = next_weights()
                for cc in range(2):
                    j = 2 * (g - G_GB) + cc
                    b = proj_B(ws_, cc, cf)
                    u = rot["g"] % 2
                    rot["g"] += 1
                    OP("act", "activation", [PB[b]], [Bgbs[u]], out=gbs[u][:, cf], in_=bank(b)[:, cf], func=AF.Silu)
                    bp = nextpair()
                    pv = pp[bp // 2][:, :].rearrange("p (n f) -> p n f", n=4)
                    for n in range(nf0, NB):
                        MM([BbigA, BwsT], [PB[bp + n // 2]], pv[:, n, :], zv_g[:, n, j * 128:(j + 1) * 128],
                           wsT[:, 2 * j * 128:(2 * j + 2) * 128], True, False)
                        MM([Bones, Bbs], [PB[bp + n // 2]], pv[:, n, :], ones_bf[0:2, :],
                           bs[0:2, 2 * j * 128:(2 * j + 2) * 128], False, True)
                    for half in range(2):
                        rows = slice(64 * half, 64 * half + 64)
                        OP("dve", "tensor_tensor", [PB[bp], PB[bp + 1], BbigB], [Bt1[u]],
                           out=t1[u][rows, cf].rearrange("p (n t) -> p n t", n=nbf),
                           in0=pv[rows, nf0:4, half * 128:(half + 1) * 128],
                           in1=zu_g[rows, j, cf].rearrange("p (n t) -> p n t", n=nbf), op=MUL)
                    OP("pool", "tensor_tensor", [Bt1[u], Bgbs[u]], [Bmix[8 + j]], out=mixT[:, 8 + j, cf], in0=t1[u][:, cf],
                       in1=gbs[u][:, cf], op=MUL)
            for g in range(NG_OUT):
                ws_ = next_weights()
                for cc in range(2):
                    c = 2 * g + cc
                    b = nextbank()
                    for ec in range(KC):
                        MM([Bws[ws_], Bmix[ec]], [PB[b]], bank(b)[:, cf], wslot[ws_][:, ec, cc * 128:(cc + 1) * 128],
                           mixT[:, ec, cf], ec == 0, ec == KC - 1)
                    OP("dve", "tensor_tensor", [BxT[c], PB[b]], [BxT[c]], out=xT[:, c, cf], in0=xT[:, c, cf],
                       in1=bank(b)[:, cf], op=ADD)

        for ti in range(NT + 1):
            xsrc = A["xh"] if ti == 0 else A["x"][(ti - 1) * T:ti * T, :]
            for n in range(NB):
                for hh in range(2):
                    s = rot["st"] % NST
                    rot["st"] += 1
                    DMA("st%d" % s, [], stgB(s), out=stg(s), in_=xsrc[n * 128:(n + 1) * 128, hh * 1024:(hh + 1) * 1024])
                    OP("act", "activation", stgB(s), [Bjunk, Bssq], out=junk[:], in_=stg(s), func=AF.Square,
                       accum_out=ssq[:, 2 * n + hh:2 * n + hh + 1])
                    for q4 in range(2):
                        b = nextbank()
                        for i in range(4):
                            j = q4 * 4 + i
                            S.op("pe", lambda e, b=b, i=i, j=j, s=s: e.transpose(
                                out=bank(b)[:, i * 128:(i + 1) * 128], in_=stg(s)[:, j * 128:(j + 1) * 128],
                                identity=cm[:, 0, :]), reads=stgB(s) + [Bcm], writes=[PB[b]])
                        c0 = hh * 8 + q4 * 4
                        eng = "dve" if (q4 == 0) else "act"
                        dst = xT[:, c0:c0 + 4, n * 128:(n + 1) * 128]
                        srcp = bank(b).rearrange("p (i t) -> p i t", i=4)
                        if eng == "dve":
                            OP("dve", "tensor_copy", [PB[b]], BxT[c0:c0 + 4], out=dst, in_=srcp)
                        else:
                            OP("act", "activation", [PB[b]], BxT[c0:c0 + 4], out=dst, in_=srcp, func=AF.Copy)

            OP("dve", "tensor_tensor", [Bssq], [Btot], out=tot[:], in0=ssq[:, 0:8:2], in1=ssq[:, 1:8:2], op=ALU.add)
            b0 = nextbank()
            for n in range(NB):
                w = n % 2
                OP("dve", "tensor_scalar", [Bcm, Btot], [Bdgs[w]], out=dgs[w][:], in0=cm[:, 0, :], scalar1=tot[:, n:n + 1],
                   scalar2=None, op0=ALU.mult)
                MM([Bdgs[w], Bcm], [PB[b0]], bank(b0)[:, n * 128:(n + 1) * 128], cm[:, 1, :], dgs[w][:], True, True)
            l0bank["b"] = b0
            for l in range(DEPTH):
                tile_layer(ti, l)

            if ti > 0:
                for n in range(NB):
                    for hh in range(2):
                        s = rot["st"] % NST
                        rot["st"] += 1
                        for q4 in range(2):
                            b = nextbank()
                            for i in range(4):
                                c = hh * 8 + q4 * 4 + i
                                S.op("pe", lambda e, b=b, i=i, c=c, n=n: e.transpose(
                                    out=bank(b)[:, i * 128:(i + 1) * 128], in_=xT[:, c, n * 128:(n + 1) * 128],
                                    identity=cm[:, 0, :]), reads=[BxT[c], Bcm], writes=[PB[b]])
                            dst = stg(s)[:, q4 * 512:(q4 + 1) * 512]
                            if q4 == 0:
                                OP("dve", "tensor_copy", [PB[b]], stgB(s), out=dst, in_=bank(b))
                            else:
                                OP("act", "activation", [PB[b]], stgB(s), out=dst, in_=bank(b), func=AF.Copy)
                        r0 = (ti - 1) * T + n * 128
                        DMA("st%d" % s, stgB(s), [], out=A["out"][r0:r0 + 128, hh * 1024:(hh + 1) * 1024], in_=stg(s))
        S.emit()


def build_nc(NT=8, DEPTH=4):
    nc = bass.Bass("TRN2", target_bir_lowering=False)
    A = {}

    def inp(name, shape, dt=F32):
        A[name] = nc.dram_tensor(name, shape, dt, kind="ExternalInput").ap()

    inp("x", [NT * T, D])
    inp("xh", [T, D])
    inp("hv", [128, 1])
    inp("norm_g", [DEPTH, D])
    inp("w_in", [DEPTH, D, DIN])
    inp("q_norm", [DEPTH, 64])
    inp("k_norm", [DEPTH, 64])
    inp("sinks", [DEPTH, 16])
    inp("w_s", [DEPTH, 16, 128, 128])
    inp("b_s", [DEPTH, 16, 128])
    inp("w_out", [DEPTH, D, D])
    inp("cmat", [128, 4, 128])
    inp("wgt", [128, 4096])
    A["out"] = nc.dram_tensor("out", [NT * T, D], F32, kind="ExternalOutput").ap()
    for name, shape, dt in (("scr_in", [DEPTH, D, DIN], BF16), ("scr_out", [DEPTH, D, D], BF16),
                            ("ws_scr", [DEPTH, 128, 2048], BF16), ("bs_scr", [2, DEPTH, 2048], BF16), ("es_scr", [2, DEPTH, 2048], BF16),
                            ("wgt_scr", [128, 4096], BF16), ("vec_scr", [128, 72], F32)):
        A[name] = nc.dram_tensor(name, shape, dt, kind="Internal").ap()
    _prologue(nc, A, DEPTH)
    _main(nc, A, DEPTH, NT)
    return nc


def make_in_maps(x, params, NT, n_pairs):
    cm, wgt = _host_consts()
    maps = []
    for b in range(n_pairs):
        for h in range(2):
            own = np.ascontiguousarray(x[b, h * NT * T:(h + 1) * NT * T, :])
            if h == 1:
                xh = np.ascontiguousarray(x[b, NT * T - T:NT * T, :])
            else:
                xh = np.zeros((T, D), np.float32)
            m = {"x": own, "xh": xh, "hv": np.full((128, 1), float(h), np.float32), "cmat": cm, "wgt": wgt}
            m.update(params)
            maps.append(m)
    return maps


_NC_CACHE = {}


def kernel(x, norm_g, w_in, q_norm, k_norm, sinks, w_s, b_s, w_out):
    NT, DEPTH = 8, 4
    x = np.asarray(x, np.float32)
    params = {"norm_g": np.ascontiguousarray(norm_g, np.float32), "w_in": np.ascontiguousarray(w_in, np.float32),
              "q_norm": np.ascontiguousarray(q_norm, np.float32), "k_norm": np.ascontiguousarray(k_norm, np.float32),
              "sinks": np.ascontiguousarray(sinks, np.float32), "w_s": np.ascontiguousarray(w_s, np.float32),
              "b_s": np.ascontiguousarray(b_s, np.float32), "w_out": np.ascontiguousarray(w_out, np.float32)}
    key = (NT, DEPTH)
    if key not in _NC_CACHE:
        _NC_CACHE[key] = build_nc(NT, DEPTH)
    nc = _NC_CACHE[key]
    maps = make_in_maps(x, params, NT, 4)
    res = run_bass_kernel_spmd(nc, maps, core_ids=list(range(8)))
    out = np.empty_like(x)
    for c in range(8):
        b, h = c // 2, c % 2
        out[b, h * NT * T:(h + 1) * NT * T, :] = res.results[c]["out"]
    return out
```
